# Optimizing a Trainium2 kernel written in Bass

```python
import math
import jax, jax.numpy as jnp
from jax import lax
import numpy as np

D_MODEL = 2048
BATCH = 4
SEQ = 4096
DEPTH = 4

CHUNK = 64

D_MIX = 2 * D_MODEL
POOL_W = D_MIX // 2
SSD_W = D_MIX - POOL_W

POOL_WINDOWS = (2, 4, 8, 16)
POOL_GROUPS = len(POOL_WINDOWS)
POOL_GROUP_W = POOL_W // POOL_GROUPS
MAX_WINDOW = max(POOL_WINDOWS)

SSD_HEAD_DIM = 64
SSD_HEADS = SSD_W // SSD_HEAD_DIM
SSD_STATE = 128
SSD_GROUPS = 4
SSD_HEADS_PER_GROUP = SSD_HEADS // SSD_GROUPS
CONV_WIDTH = 4
CONV_DIM = SSD_W + 2 * SSD_GROUPS * SSD_STATE

D_IN_PROJ = 2 * POOL_W + 2 * SSD_W + 2 * SSD_GROUPS * SSD_STATE + SSD_HEADS

NORM_EPS = 1e-6

kernel_name = "hybrid_pool_ssd_sandwich_trunk"


def rms_norm(x, w):
    xf = x.astype(jnp.float32)
    var = jnp.mean(xf * xf, axis=-1, keepdims=True)
    return (xf * lax.rsqrt(var + NORM_EPS) * w.astype(jnp.float32)).astype(x.dtype)


def pool_mixer(u, mix_w, scale):
    b, l, _ = u.shape
    uf = u.astype(jnp.float32)
    cs = jnp.cumsum(uf, axis=1)
    csp = jnp.pad(cs, ((0, 0), (MAX_WINDOW, 0), (0, 0)))
    t = jnp.arange(l)
    outs = []
    for g, w in enumerate(POOL_WINDOWS):
        sl = slice(g * POOL_GROUP_W, (g + 1) * POOL_GROUP_W)
        win_sum = csp[:, MAX_WINDOW:, sl] - csp[:, MAX_WINDOW - w:MAX_WINDOW - w + l, sl]
        cnt = jnp.minimum(t + 1, w).astype(jnp.float32)[None, :, None]
        outs.append(win_sum / cnt - uf[:, :, sl])
    pooled = jnp.stack(outs, axis=2).astype(u.dtype)
    mixed = jnp.einsum('blgc,gcd->blgd', pooled, mix_w)
    return mixed.reshape(b, l, POOL_W) * scale


def causal_dwconv(u, w, bias):
    y = lax.conv_general_dilated(
        u, w[:, None, :], window_strides=(1,), padding=[(CONV_WIDTH - 1, 0)],
        dimension_numbers=('NWC', 'WIO', 'NWC'), feature_group_count=u.shape[-1])
    return y + bias


def ssd_scan(xh, dt, a, bm, cm):
    b, l, h, p = xh.shape
    c = l // CHUNK
    G, R, N, Q = SSD_GROUPS, SSD_HEADS_PER_GROUP, SSD_STATE, CHUNK
    xdt = (xh * dt[..., None]).reshape(b, c, Q, G, R, p)
    adt = (dt * a).reshape(b, c, Q, G, R).transpose(0, 3, 4, 1, 2)
    acs = jnp.cumsum(adt, axis=-1)
    bm = bm.reshape(b, c, Q, G, N)
    cm = cm.reshape(b, c, Q, G, N)
    mask = jnp.tril(jnp.ones((Q, Q), dtype=bool))
    seg = acs[..., :, None] - acs[..., None, :]
    decay = jnp.exp(jnp.where(mask, seg, -jnp.inf))
    scores = jnp.einsum('bclgn,bcsgn->bgcls', cm, bm)
    y_diag = jnp.einsum('bgcls,bgrcls,bcsgrp->bclgrp', scores, decay, xdt)
    decay_states = jnp.exp(acs[..., -1:] - acs)
    states = jnp.einsum('bcsgn,bgrcs,bcsgrp->bcgrpn', bm, decay_states, xdt)
    chunk_decay = jnp.exp(acs[..., -1])

    def step(carry, inp):
        st, dec = inp
        return carry * dec[..., None, None] + st, carry

    init = jnp.zeros((b, G, R, p, N), dtype=jnp.float32)
    _, prev = lax.scan(step, init, (jnp.moveaxis(states, 1, 0), jnp.moveaxis(chunk_decay, 3, 0)))
    prev = jnp.moveaxis(prev, 0, 1)
    y_off = jnp.einsum('bclgn,bcgrpn,bgrcl->bclgrp', cm, prev, jnp.exp(acs))
    return (y_diag + y_off).reshape(b, l, h, p)


def ssd_branch(z, xbc_raw, dt_raw, conv_w, conv_b, dt_bias, a_log, d_skip, norm_w):
    b, l, _ = z.shape
    xbc = jax.nn.silu(causal_dwconv(xbc_raw, conv_w, conv_b))
    xs = xbc[..., :SSD_W]
    bm = xbc[..., SSD_W:SSD_W + SSD_GROUPS * SSD_STATE].reshape(b, l, SSD_GROUPS, SSD_STATE)
    cm = xbc[..., SSD_W + SSD_GROUPS * SSD_STATE:].reshape(b, l, SSD_GROUPS, SSD_STATE)
    dt = jax.nn.softplus(dt_raw.astype(jnp.float32) + dt_bias.astype(jnp.float32))
    a = -jnp.exp(a_log.astype(jnp.float32))
    xh = xs.reshape(b, l, SSD_HEADS, SSD_HEAD_DIM).astype(jnp.float32)
    y = ssd_scan(xh, dt, a, bm.astype(jnp.float32), cm.astype(jnp.float32))
    y = y + d_skip.astype(jnp.float32)[:, None] * xh
    y = y.reshape(b, l, SSD_W) * jax.nn.silu(z.astype(jnp.float32))
    yg = y.reshape(b, l, SSD_GROUPS, SSD_W // SSD_GROUPS)
    yg = yg * lax.rsqrt(jnp.mean(yg * yg, axis=-1, keepdims=True) + NORM_EPS)
    y = yg.reshape(b, l, SSD_W) * norm_w.astype(jnp.float32)
    return y.astype(z.dtype)


def setup_inputs(seed: int = 0) -> dict:
    key = jax.random.key(seed)
    ks = jax.random.split(key, 16)
    f32 = jnp.float32
    x = jax.random.normal(ks[0], (BATCH, SEQ, D_MODEL), f32)
    pre_norm_w = 1.0 + 0.02 * jax.random.normal(ks[1], (DEPTH, D_MODEL), f32)
    w_in = jax.random.normal(ks[2], (DEPTH, D_MODEL, D_IN_PROJ), f32) * D_MODEL ** -0.5
    pool_mix_w = jax.random.normal(ks[3], (DEPTH, POOL_GROUPS, POOL_GROUP_W, POOL_GROUP_W), f32) * POOL_GROUP_W ** -0.5
    pool_scale = 1.0 + 0.02 * jax.random.normal(ks[4], (DEPTH, POOL_W), f32)
    conv_w = jax.random.normal(ks[5], (DEPTH, CONV_WIDTH, CONV_DIM), f32) * CONV_WIDTH ** -0.5
    conv_b = 0.02 * jax.random.normal(ks[6], (DEPTH, CONV_DIM), f32)
    u = jax.random.uniform(ks[7], (DEPTH, SSD_HEADS), f32)
    dt0 = jnp.exp(u * (math.log(0.1) - math.log(0.001)) + math.log(0.001))
    dt_bias = dt0 + jnp.log(-jnp.expm1(-dt0))
    a_log = jnp.log(jax.random.uniform(ks[8], (DEPTH, SSD_HEADS), f32, minval=1.0, maxval=16.0))
    d_skip = 1.0 + 0.02 * jax.random.normal(ks[9], (DEPTH, SSD_HEADS), f32)
    ssd_norm_w = 1.0 + 0.02 * jax.random.normal(ks[10], (DEPTH, SSD_W), f32)
    w_out = jax.random.normal(ks[11], (DEPTH, D_MIX, D_MODEL), f32) * D_MIX ** -0.5
    post_norm_w = 1.0 + 0.02 * jax.random.normal(ks[12], (DEPTH, D_MODEL), f32)
    return {"x": x, "pre_norm_w": pre_norm_w, "w_in": w_in, "pool_mix_w": pool_mix_w,
            "pool_scale": pool_scale, "conv_w": conv_w, "conv_b": conv_b,
            "dt_bias": dt_bias, "a_log": a_log, "d_skip": d_skip,
            "ssd_norm_w": ssd_norm_w, "w_out": w_out, "post_norm_w": post_norm_w}


def reference(x, pre_norm_w, w_in, pool_mix_w, pool_scale, conv_w, conv_b,
              dt_bias, a_log, d_skip, ssd_norm_w, w_out, post_norm_w):
    o1 = POOL_W
    o2 = o1 + POOL_W
    o3 = o2 + SSD_W
    o4 = o3 + CONV_DIM
    for layer in range(DEPTH):
        h = rms_norm(x, pre_norm_w[layer])
        proj = jnp.einsum('bld,de->ble', h, w_in[layer])
        pool_u = proj[..., :o1]
        pool_gate = proj[..., o1:o2]
        ssd_z = proj[..., o2:o3]
        ssd_xbc = proj[..., o3:o4]
        ssd_dt = proj[..., o4:]
        y_pool = pool_mixer(pool_u, pool_mix_w[layer], pool_scale[layer]) * jax.nn.silu(pool_gate)
        y_ssd = ssd_branch(ssd_z, ssd_xbc, ssd_dt, conv_w[layer], conv_b[layer],
                           dt_bias[layer], a_log[layer], d_skip[layer], ssd_norm_w[layer])
        mixed = jnp.concatenate([y_pool.astype(x.dtype), y_ssd.astype(x.dtype)], axis=-1)
        out = jnp.einsum('ble,ed->bld', mixed, w_out[layer])
        x = x + rms_norm(out, post_norm_w[layer])
    return x
```

```python
import numpy as np
from contextlib import ExitStack
import concourse.bass as bass
import concourse.mybir as mybir
from concourse.bass_utils import run_bass_kernel_spmd

F32, BF16 = mybir.dt.float32, mybir.dt.bfloat16
AF = mybir.ActivationFunctionType
ALU = mybir.AluOpType

D = 2048
NKC = 16
T = 512
DIN = 9248
EPS = 1e-6
PRE, POST, PSC, CW, CB, NW, DSK, DTB, ALOG = 0, 16, 32, 48, 144, 168, 184, 200, 201
NPAR = 233

ENGS = ("pe", "act", "dve", "pool", "sp")
SEM_LIMIT = 8000


class Op:
    __slots__ = ("eng", "fn", "deps", "signal", "sig", "dma", "idx")

    def __init__(self, eng, fn):
        self.eng = eng
        self.fn = fn
        self.deps = []
        self.signal = False
        self.sig = None
        self.dma = None
        self.idx = 0


class _Rec:
    def __getattr__(self, name):
        def f(*a, **k):
            self.call = (name, a, k)
            return self
        return f


class Sched:
    def batch(self, ops):
        n = max(o.dma[1] for o in ops)
        for o in ops:
            o.dma = (o.dma[0], n)

    def __init__(self, same_engine_sync=True):
        self.ops = {e: [] for e in ENGS}
        self.last_w = {}
        self.readers = {}
        self.dma_cnt = {}
        self.same_engine_sync = same_engine_sync

    def _deps(self, op, reads, writes):
        deps = {}

        def add(p):
            if p is None or p is op:
                return
            if (p.dma is None and op.dma is None and p.eng == op.eng
                    and (op.eng == "pe" or not self.same_engine_sync)):
                return
            deps[id(p)] = p

        for b in reads:
            add(self.last_w.get(b))
        for b in writes:
            add(self.last_w.get(b))
            for r in self.readers.get(b, ()):
                add(r)
        best = {}
        out = []
        for p in deps.values():
            if p.dma is not None:
                out.append(p)
            else:
                q = best.get(p.eng)
                if q is None or p.idx > q.idx:
                    best[p.eng] = p
        out.extend(best.values())
        op.deps = out
        for b in writes:
            self.last_w[b] = op
            self.readers[b] = []
        for b in reads:
            self.readers.setdefault(b, []).append(op)

    def add(self, eng, fn, reads=(), writes=()):
        rec = _Rec()
        fn(rec)
        name, a, k = rec.call
        op = Op(eng, lambda e, name=name, a=a, k=k: getattr(e, name)(*a, **k))
        op.idx = len(self.ops[eng])
        self._deps(op, reads, writes)
        self.ops[eng].append(op)
        return op

    def dma(self, eng, semkey, out, in_, reads=(), writes=()):
        op = Op(eng, lambda e: e.dma_start(out=out, in_=in_))
        op.idx = len(self.ops[eng])
        n = self.dma_cnt.get(semkey, 0) + 16
        self.dma_cnt[semkey] = n
        op.dma = (semkey, n)
        self._deps(op, reads, writes)
        self.ops[eng].append(op)
        return op

    def emit(self, nc, final_waits=()):
        for e in ENGS:
            for op in self.ops[e]:
                for p in op.deps:
                    if p.dma is None:
                        p.signal = True
        nsem = {}
        for e in ENGS:
            k = 0
            for op in self.ops[e]:
                if op.dma is None and op.signal:
                    op.sig = (k // SEM_LIMIT, k % SEM_LIMIT + 1)
                    k += 1
            nsem[e] = (k + SEM_LIMIT - 1) // SEM_LIMIT
        with ExitStack() as st:
            sems = {e: [st.enter_context(nc.semaphore(f"s_{e}_{i}")) for i in range(nsem[e])] for e in ENGS}
            dsems = {k: st.enter_context(nc.semaphore(f"d_{k}")) for k in self.dma_cnt}
            block = st.enter_context(nc.Block())
            engmap = {"pe": "tensor", "act": "scalar", "dve": "vector", "pool": "gpsimd", "sp": "sync"}

            def make(e):
                def body(eng):
                    waited = {}
                    dwaited = {}
                    for op in self.ops[e]:
                        for p in op.deps:
                            if p.dma is not None:
                                k, v = p.dma
                                if dwaited.get(k, 0) < v:
                                    eng.wait_ge(dsems[k], v)
                                    dwaited[k] = v
                            else:
                                si, v = p.sig
                                key = (p.eng, si)
                                if waited.get(key, 0) < v:
                                    eng.wait_ge(sems[p.eng][si], v)
                                    waited[key] = v
                        ins = op.fn(eng)
                        if op.dma is not None:
                            ins.then_inc(dsems[op.dma[0]], 16)
                        elif op.signal:
                            ins.then_inc(sems[e][op.sig[0]], 1)
                    if e == "sp":
                        for k in final_waits:
                            eng.wait_ge(dsems[k], self.dma_cnt[k])
                return body

            for e in ENGS:
                getattr(block, engmap[e])(make(e))


def build(NT, NSTEP, same_engine_sync=True):
    assert NT % T == 0
    NTILE = NT // T
    nc = bass.Bass("TRN2", target_bir_lowering=False)
    x_in = nc.dram_tensor("x", [128, NKC, NT], F32, kind="ExternalInput").ap()
    y_out = nc.dram_tensor("y", [128, NKC, NT], F32, kind="ExternalOutput").ap()
    w_in = nc.dram_tensor("w_in", [NSTEP, D, DIN], F32, kind="ExternalInput").ap()
    w_out = nc.dram_tensor("w_out", [NSTEP, 2 * D, D], F32, kind="ExternalInput").ap()
    mixw = nc.dram_tensor("mixw", [NSTEP, 4, 512, 512], F32, kind="ExternalInput").ap()
    par = nc.dram_tensor("par", [NSTEP, 128, NPAR], F32, kind="ExternalInput").ap()
    cst = nc.dram_tensor("cst", [128, 64], F32, kind="ExternalInput").ap()

    with ExitStack() as st:
        def sb(name, shape, dt):
            return st.enter_context(nc.sbuf_tensor(name, shape, dt))

        def ps(name, shape, dt):
            return st.enter_context(nc.psum_tensor(name, shape, dt))

        S = Sched(same_engine_sync)
        A = S.add

        big = sb("big", [128, 8192], F32)
        bigc = big[:].rearrange("p (c t) -> p c t", c=16)
        szT = big[:, 0:2048].rearrange("p (c t) -> p c t", c=4)
        BT = big[:, 2048:3072].bitcast(BF16).rearrange("p (g t) -> p g t", g=4)
        CT = big[:, 3072:4096].bitcast(BF16).rearrange("p (g t) -> p g t", g=4)
        xsT = big[:, 4096:5120].bitcast(BF16).rearrange("p (c t) -> p c t", c=4)
        ub = [big[:, 0:528], big[:, 528:1056]]
        pa = big[:, 1056:1584]
        pb = big[:, 1584:2112]
        plT = big[:, 2112:3136].bitcast(BF16).rearrange("p (c t) -> p c t", c=4)
        sg = [big[:, 3136:3648], big[:, 3648:4160]]

        def segs(lo, hi):
            return [("big", i) for i in range(lo // 512, (hi - 1) // 512 + 1)]

        K_szT = [segs(c * 512, (c + 1) * 512) for c in range(4)]
        K_BT = [segs(2048 + g * 256, 2048 + (g + 1) * 256) for g in range(4)]
        K_CT = [segs(3072 + g * 256, 3072 + (g + 1) * 256) for g in range(4)]
        K_xsT = [segs(4096 + c * 256, 4096 + (c + 1) * 256) for c in range(4)]
        K_ub = [segs(0, 528), segs(528, 1056)]
        K_pa = segs(1056, 1584)
        K_pb = segs(1584, 2112)
        K_plT = [segs(2112 + c * 256, 2112 + (c + 1) * 256) for c in range(4)]
        K_sg = [segs(3136, 3648), segs(3648, 4160)]
        K_big = [segs(c * 512, (c + 1) * 512) for c in range(16)]

        hT = sb("hT", [128, NKC, T], BF16)
        mixT = sb("mixT", [128, 32, T], BF16)
        NWB = 3
        wbuf = [sb(f"wbuf{i}", [128, NKC, 256], BF16) for i in range(NWB)]
        wo = [sb(f"wo{i}", [128, 32, 128], BF16) for i in range(2)]
        mw = [sb(f"mw{i}", [128, 4, 512], BF16) for i in range(2)]
        wdt = sb("wdt", [128, NKC, 32], BF16)
        part = [sb(f"par{i}", [128, NPAR], F32) for i in range(2)]
        cstt = sb("cstt", [128, 64], F32)
        ident = sb("ident", [128, 128], BF16)
        identF = sb("identF", [128, 128], F32)
        mle = sb("mle", [128, 64], F32)
        mgt2 = sb("mgt2", [128, 128], F32)
        mle2 = sb("mle2", [128, 128], F32)
        onesc = sb("onesc", [128, 2, 128], F32)
        ones_bf = sb("ones_bf", [128, 128], BF16)
        a_bc = sb("a_bc", [128, 32], F32)
        rs = sb("rs", [128, T], F32)
        sq = [sb(f"sq{i}", [128, T], BF16) for i in range(2)]
        raw = [sb(f"raw{i}", [128, T + 3], F32) for i in range(2)]
        acc = [sb(f"acc{i}", [128, T], F32) for i in range(2)]
        carry = sb("carry", [128, 24, 3], F32)
        pcarry = sb("pcarry", [128, 16, 16], F32)
        dtT = sb("dtT", [32, T], F32)
        dt_tok = sb("dt_tok", [128, 4, 32], F32)
        adt_tok = sb("adt_tok", [128, 4, 32], F32)
        Sst = sb("Sst", [128, 4, 512], F32)
        Sbf = sb("Sbf", [128, 4, 512], BF16)
        scm = [sb(f"scm{i}", [128, 64], F32) for i in range(2)]
        Ap = [sb(f"Ap{i}", [128, 8, 64], F32) for i in range(2)]
        Eb = [sb(f"E{i}", [128, 8, 64], F32) for i in range(2)]
        MT = [sb(f"MT{i}", [128, 8, 64], BF16) for i in range(2)]
        xdt = [sb(f"xdt{i}", [128, 8, 64], BF16) for i in range(2)]
        xdtd = [sb(f"xdtd{i}", [128, 8, 64], BF16) for i in range(2)]
        Btok = [sb(f"Btok{i}", [128, 128], BF16) for i in range(2)]
        decacs = [sb(f"decacs{i}", [128, 24], F32) for i in range(2)]
        yo = [sb(f"yo{i}", [128, 8, 64], F32) for i in range(2)]
        ytok = [sb(f"ytok{i}", [128, 512], BF16) for i in range(2)]
        tmpg = [sb(f"tmpg{i}", [128, 4, 128], F32) for i in range(2)]
        tmp16 = sb("tmp16", [128, 16], F32)
        xr = [sb(f"xr{i}", [128, T], F32) for i in range(2)]

        pj = [ps(f"pj{i}", [128, 512], F32) for i in range(2)]
        pstat = ps("pstat", [128, 512], F32)
        psmall = ps("psmall", [128, 512], F32)
        ptr = ps("ptr", [128, 1024], BF16)
        py = ps("py", [128, 512], F32)
        pyoff = ps("pyoff", [128, 512], F32)
        pst = ps("pst", [128, 512], F32)

        A("pool", lambda e: e.memset(ident[:], 1.0), writes=["ident"])
        A("pool", lambda e: e.affine_select(out=ident[:], in_=ident[:], pattern=[[1, 128]], compare_op=ALU.is_equal,
                                            fill=0.0, base=0, channel_multiplier=-1), reads=["ident"], writes=["ident"])
        A("pool", lambda e: e.memset(identF[:], 1.0), writes=["identF"])
        A("pool", lambda e: e.affine_select(out=identF[:], in_=identF[:], pattern=[[1, 128]], compare_op=ALU.is_equal,
                                            fill=0.0, base=0, channel_multiplier=-1), reads=["identF"], writes=["identF"])
        A("pool", lambda e: e.memset(mle[:], 1.0), writes=["mle"])
        for c in range(2):
            v = mle[c * 64:(c + 1) * 64, :]
            A("pool", lambda e, v=v: e.affine_select(out=v, in_=v, pattern=[[1, 64]], compare_op=ALU.is_ge, fill=0.0,
                                                     base=0, channel_multiplier=-1), reads=["mle"], writes=["mle"])
        A("pool", lambda e: e.memset(mgt2[:], 0.0), writes=["mgt2"])
        A("pool", lambda e: e.memset(mle2[:], 0.0), writes=["mle2"])
        A("pool", lambda e: e.memset(onesc[:], 0.0), writes=["onesc"])
        for c in range(2):
            v = mgt2[c * 64:(c + 1) * 64, c * 64:(c + 1) * 64]
            A("pool", lambda e, v=v: e.memset(v, 1.0), reads=["mgt2"], writes=["mgt2"])
            A("pool", lambda e, v=v: e.affine_select(out=v, in_=v, pattern=[[-1, 64]], compare_op=ALU.is_ge, fill=0.0,
                                                     base=-1, channel_multiplier=1), reads=["mgt2"], writes=["mgt2"])
            v2 = mle2[c * 64:(c + 1) * 64, c * 64:(c + 1) * 64]
            A("pool", lambda e, v2=v2: e.memset(v2, 1.0), reads=["mle2"], writes=["mle2"])
            A("pool", lambda e, v2=v2: e.affine_select(out=v2, in_=v2, pattern=[[1, 64]], compare_op=ALU.is_ge, fill=0.0,
                                                       base=0, channel_multiplier=-1), reads=["mle2"], writes=["mle2"])
            v3 = onesc[c * 64:(c + 1) * 64, c, :]
            A("pool", lambda e, v3=v3: e.memset(v3, 1.0), reads=["onesc"], writes=["onesc"])
        A("pool", lambda e: e.memset(ones_bf[:], 1.0), writes=["ones_bf"])
        S.dma("sp", "cst", cstt[:], cst, writes=["cstt"])

        wslot = [0]
        woslot = [0]
        mwslot = [0]
        pjslot = [0]
        sqslot = [0]
        rawslot = [0]

        def rs_from_stat(n_feat, key_rs="rs"):
            A("act", lambda e: e.activation(out=rs[:], in_=pstat[:], func=AF.Ln, bias=EPS, scale=1.0 / n_feat),
              reads=["pstat"], writes=[key_rs])
            A("act", lambda e: e.activation(out=rs[:], in_=rs[:], func=AF.Exp, scale=-0.5), reads=[key_rs], writes=[key_rs])

        for step in range(NSTEP):
            pt = part[step % 2]
            kpar = ("par", step % 2)
            S.dma("sp", f"par{step % 2}", pt[:], par[step], writes=[kpar])
            A("act", lambda e, pt=pt: e.activation(out=a_bc[:], in_=pt[:, ALOG:ALOG + 32], func=AF.Exp), reads=[kpar], writes=["a_bc"])
            A("act", lambda e: e.activation(out=a_bc[:], in_=a_bc[:], func=AF.Copy, scale=-1.0), reads=["a_bc"], writes=["a_bc"])
            A("pool", lambda e: e.memset(Sst[:], 0.0), writes=[("S", g) for g in range(4)])
            A("pool", lambda e: e.memset(Sbf[:], 0.0), writes=[("Sbf", g) for g in range(4)])
            A("pool", lambda e: e.memset(carry[:], 0.0), writes=[("carry", i) for i in range(24)])
            A("pool", lambda e: e.memset(pcarry[:], 0.0), writes=[("pcarry", i) for i in range(16)])
            src = x_in if step == 0 else y_out

            for tt in range(NTILE):
                tsl = slice(tt * T, (tt + 1) * T)
                S.batch([S.dma("sp", "xin", bigc[:, kc, :], src[:, kc, tsl], reads=[("xd", tt, kc)], writes=K_big[kc])
                         for kc in range(NKC)])
                for kc in range(NKC):
                    q = sq[sqslot[0] % 2]
                    kq = ("sq", sqslot[0] % 2)
                    sqslot[0] += 1
                    A("act", lambda e, q=q, kc=kc: e.activation(out=q[:], in_=bigc[:, kc, :], func=AF.Square),
                      reads=K_big[kc], writes=[kq])
                    A("pe", lambda e, q=q, kc=kc: e.matmul(pstat[:], lhsT=ones_bf[:], rhs=q[:], start=(kc == 0), stop=(kc == NKC - 1)),
                      reads=[kq, "ones_bf"], writes=["pstat"])
                rs_from_stat(float(D))
                for kc in range(NKC):
                    A("dve", lambda e, kc=kc, pt=pt: e.scalar_tensor_tensor(out=hT[:, kc, :], in0=bigc[:, kc, :],
                                                                          scalar=pt[:, PRE + kc:PRE + kc + 1], in1=rs[:],
                                                                          op0=ALU.mult, op1=ALU.mult),
                      reads=K_big[kc] + ["rs", kpar], writes=[("hT", kc)])

                def load_wblk(c0, width):
                    slot = wslot[0] % NWB
                    wslot[0] += 1
                    S.dma("pool", f"w{slot}", wbuf[slot][:, :, 0:width],
                          w_in[step, :, c0:c0 + width].rearrange("(kc p) f -> p kc f", p=128),
                          writes=[("wbuf", slot)])
                    return slot

                def proj_chunk(wt, f0, fw, kw):
                    b = pjslot[0] % 2
                    pjslot[0] += 1
                    for kc in range(NKC):
                        A("pe", lambda e, kc=kc, b=b: e.matmul(pj[b][0:fw, :], lhsT=wt[:, kc, f0:f0 + fw], rhs=hT[:, kc, :],
                                                                start=(kc == 0), stop=(kc == NKC - 1)),
                          reads=[kw, ("hT", kc)], writes=[("pj", b)])
                    return pj[b], ("pj", b)

                def conv_chunk(pp, kp, ci, dst, kdst):
                    r = raw[rawslot[0] % 2]
                    ac = acc[rawslot[0] % 2]
                    kr = ("raw", rawslot[0] % 2)
                    ka = ("acc", rawslot[0] % 2)
                    rawslot[0] += 1
                    A("act", lambda e: e.activation(out=r[:, 3:3 + T], in_=pp[:, :], func=AF.Copy), reads=[kp], writes=[kr])
                    A("dve", lambda e: e.tensor_copy(out=r[:, 0:3], in_=carry[:, ci, :]), reads=[("carry", ci)], writes=[kr])
                    A("dve", lambda e: e.tensor_copy(out=carry[:, ci, :], in_=r[:, T:T + 3]), reads=[kr], writes=[("carry", ci)])
                    A("dve", lambda e: e.tensor_scalar(out=ac[:], in0=r[:, 0:T], scalar1=pt[:, CW + ci * 4:CW + ci * 4 + 1],
                                                       scalar2=pt[:, CB + ci:CB + ci + 1], op0=ALU.mult, op1=ALU.add),
                      reads=[kr, kpar], writes=[ka])
                    for k in range(1, 4):
                        A("dve", lambda e, k=k: e.scalar_tensor_tensor(out=ac[:], in0=r[:, k:k + T],
                                                                       scalar=pt[:, CW + ci * 4 + k:CW + ci * 4 + k + 1],
                                                                       in1=ac[:], op0=ALU.mult, op1=ALU.add),
                          reads=[kr, ka, kpar], writes=[ka])
                    A("act", lambda e: e.activation(out=dst, in_=ac[:], func=AF.Silu), reads=[ka], writes=kdst)

                S.dma("pool", "wdt", wdt[:], w_in[step, :, 9216:9248].rearrange("(kc p) f -> p kc f", p=128), writes=["wdt"])
                pp, kp = proj_chunk(wdt, 0, 32, "wdt")
                A("act", lambda e, pp=pp: e.activation(out=dtT[:], in_=pp[0:32, :], func=AF.Exp, bias=pt[0:32, DTB:DTB + 1]),
                  reads=[kp, kpar], writes=["dtT"])
                A("act", lambda e: e.activation(out=dtT[:], in_=dtT[:], func=AF.Ln, bias=1.0), reads=["dtT"], writes=["dtT"])
                for blk in range(4):
                    A("pe", lambda e, blk=blk: e.transpose(out=psmall[:, 160:192], in_=dtT[0:32, blk * 128:(blk + 1) * 128],
                                                           identity=identF[0:32, 0:32]),
                      reads=["dtT", "identF"], writes=["psmall"])
                    A("dve", lambda e, blk=blk: e.tensor_copy(out=dt_tok[:, blk, :], in_=psmall[:, 160:192]),
                      reads=["psmall"], writes=[("dt_tok", blk)])
                    A("dve", lambda e, blk=blk: e.tensor_tensor(out=adt_tok[:, blk, :], in0=dt_tok[:, blk, :], in1=a_bc[:], op=ALU.mult),
                      reads=[("dt_tok", blk), "a_bc"], writes=[("adt_tok", blk)])
                for which, base_col, dstT, kd, ci0 in (("B", 8192, BT, K_BT, 16), ("C", 8704, CT, K_CT, 20)):
                    for half in range(2):
                        slot = load_wblk(base_col + half * 256, 256)
                        for j in range(2):
                            g = half * 2 + j
                            pp, kp = proj_chunk(wbuf[slot], j * 128, 128, ("wbuf", slot))
                            conv_chunk(pp, kp, ci0 + g, dstT[:, g, :], kd[g])

                for g in range(4):
                    for half in range(2):
                        slot = load_wblk(6144 + g * 512 + half * 256, 256)
                        for j in range(2):
                            cc = half * 2 + j
                            pp, kp = proj_chunk(wbuf[slot], j * 128, 128, ("wbuf", slot))
                            conv_chunk(pp, kp, g * 4 + cc, xsT[:, cc, :], K_xsT[cc])
                    for half in range(2):
                        slot = load_wblk(4096 + g * 512 + half * 256, 256)
                        for j in range(2):
                            cc = half * 2 + j
                            pp, kp = proj_chunk(wbuf[slot], j * 128, 128, ("wbuf", slot))
                            A("act", lambda e, pp=pp, cc=cc: e.activation(out=szT[:, cc, :], in_=pp[:, :], func=AF.Silu),
                              reads=[kp], writes=K_szT[cc])
                    for blk in range(4):
                        i2 = blk % 2
                        bsl = slice(blk * 128, (blk + 1) * 128)
                        hs = slice(g * 8, (g + 1) * 8)
                        A("pe", lambda e, bsl=bsl: e.matmul(psmall[:, 0:128], lhsT=BT[:, g, bsl], rhs=CT[:, g, bsl], start=True, stop=True),
                          reads=K_BT[g] + K_CT[g], writes=["psmall"])
                        for c in range(2):
                            A("pe", lambda e, c=c, blk=blk, hs=hs: e.matmul(psmall[:, 128 + c * 8:136 + c * 8], lhsT=onesc[:, c, :],
                                                                           rhs=adt_tok[:, blk, hs], start=True, stop=True),
                              reads=[("adt_tok", blk), "onesc"], writes=["psmall"])
                        A("pe", lambda e, blk=blk, hs=hs: e.matmul(psmall[:, 144:152], lhsT=mle2[:], rhs=adt_tok[:, blk, hs], start=True, stop=True),
                          reads=[("adt_tok", blk), "mle2"], writes=["psmall"])
                        for c in range(2):
                            psl = slice(c * 64, (c + 1) * 64)
                            A("dve", lambda e, psl=psl, c=c, i2=i2: e.tensor_tensor(out=scm[i2][psl, :], in0=psmall[psl, c * 64:(c + 1) * 64],
                                                                                  in1=mle[psl, :], op=ALU.mult),
                              reads=["psmall", "mle"], writes=[("scm", i2)])
                        A("act", lambda e, i2=i2: e.activation(out=decacs[i2][:], in_=psmall[:, 128:152], func=AF.Exp),
                          reads=["psmall"], writes=[("decacs", i2)])
                        A("dve", lambda e, blk=blk, i2=i2, hs=hs: e.tensor_tensor(
                            out=Ap[i2][:], in0=adt_tok[:, blk, hs].unsqueeze(2).to_broadcast([128, 8, 64]),
                            in1=mle[:].unsqueeze(1).to_broadcast([128, 8, 64]), op=ALU.mult),
                          reads=[("adt_tok", blk), "mle"], writes=[("Ap", i2)])
                        A("pe", lambda e, i2=i2: e.matmul(pstat[:], lhsT=mgt2[:], rhs=Ap[i2][:].rearrange("p a b -> p (a b)"),
                                                          start=True, stop=True),
                          reads=[("Ap", i2), "mgt2"], writes=["pstat"])
                        A("act", lambda e, i2=i2: e.activation(out=Eb[i2][:].rearrange("p a b -> p (a b)"), in_=pstat[:], func=AF.Exp),
                          reads=["pstat"], writes=[("E", i2)])
                        A("dve", lambda e, i2=i2: e.tensor_tensor(out=MT[i2][:], in0=Eb[i2][:],
                                                                  in1=scm[i2][:].unsqueeze(1).to_broadcast([128, 8, 64]), op=ALU.mult),
                          reads=[("E", i2), ("scm", i2)], writes=[("MT", i2)])
                        for cc in range(4):
                            A("pe", lambda e, cc=cc, bsl=bsl: e.transpose(out=ptr[:, cc * 128:(cc + 1) * 128], in_=xsT[:, cc, bsl], identity=ident[:]),
                              reads=K_xsT[cc] + ["ident"], writes=["ptr"])
                        A("pe", lambda e, bsl=bsl: e.transpose(out=ptr[:, 512:640], in_=BT[:, g, bsl], identity=ident[:]),
                          reads=K_BT[g] + ["ident"], writes=["ptr"])
                        A("dve", lambda e, i2=i2, blk=blk, hs=hs: e.tensor_tensor(
                            out=xdt[i2][:], in0=ptr[:, 0:512].rearrange("p (a b) -> p a b", a=8),
                            in1=dt_tok[:, blk, hs].unsqueeze(2).to_broadcast([128, 8, 64]), op=ALU.mult),
                          reads=["ptr", ("dt_tok", blk)], writes=[("xdt", i2)])
                        A("dve", lambda e, i2=i2: e.tensor_tensor(
                            out=xdtd[i2][:], in0=xdt[i2][:], in1=Eb[i2][:, :, 63:64].to_broadcast([128, 8, 64]), op=ALU.mult),
                          reads=[("xdt", i2), ("E", i2)], writes=[("xdtd", i2)])
                        A("act", lambda e, i2=i2: e.activation(out=Btok[i2][:], in_=ptr[:, 512:640], func=AF.Copy),
                          reads=["ptr"], writes=[("Btok", i2)])
                        for c in range(2):
                            psl = slice(c * 64, (c + 1) * 64)
                            for r in range(8):
                                A("pe", lambda e, psl=psl, c=c, r=r, i2=i2: e.matmul(
                                    py[psl, r * 64:(r + 1) * 64], lhsT=MT[i2][psl, r, :], rhs=xdt[i2][psl, r, :],
                                    start=True, stop=True, tile_position=(c * 64, c * 64)),
                                  reads=[("MT", i2), ("xdt", i2)], writes=["py"])
                        for c in range(2):
                            psl = slice(c * 64, (c + 1) * 64)
                            csl = slice(blk * 128 + c * 64, blk * 128 + (c + 1) * 64)
                            A("pe", lambda e, psl=psl, csl=csl, c=c: e.matmul(pyoff[psl, :], lhsT=CT[:, g, csl], rhs=Sbf[:, g, :],
                                                                              start=True, stop=True, tile_position=(0, c * 64)),
                              reads=K_CT[g] + [("Sbf", g)], writes=["pyoff"])
                            A("pe", lambda e, psl=psl, i2=i2, c=c: e.matmul(pst[:], lhsT=Btok[i2][psl, :],
                                                                            rhs=xdtd[i2][psl, :, :].rearrange("p a b -> p (a b)"),
                                                                            start=True, stop=True, tile_position=(c * 64, 0)),
                              reads=[("Btok", i2), ("xdtd", i2)], writes=["pst"])
                            Sg = Sst[:, g, :].rearrange("p (a b) -> p a b", a=8)
                            A("dve", lambda e, Sg=Sg, i2=i2, c=c: e.tensor_tensor(
                                out=Sg, in0=Sg, in1=decacs[i2][:, c * 8:(c + 1) * 8].unsqueeze(2).to_broadcast([128, 8, 64]), op=ALU.mult),
                              reads=[("S", g), ("decacs", i2)], writes=[("S", g)])
                            A("dve", lambda e: e.tensor_tensor(out=Sst[:, g, :], in0=Sst[:, g, :], in1=pst[:], op=ALU.add),
                              reads=[("S", g), "pst"], writes=[("S", g)])
                            A("act", lambda e: e.activation(out=Sbf[:, g, :], in_=Sst[:, g, :], func=AF.Copy),
                              reads=[("S", g)], writes=[("Sbf", g)])
                        A("dve", lambda e, i2=i2: e.tensor_tensor(
                            out=yo[i2][:], in0=pyoff[:, :].rearrange("p (a b) -> p a b", a=8),
                            in1=decacs[i2][:, 16:24].unsqueeze(2).to_broadcast([128, 8, 64]), op=ALU.mult),
                          reads=["pyoff", ("decacs", i2)], writes=[("yo", i2)])
                        A("dve", lambda e, i2=i2: e.tensor_tensor(out=ytok[i2][:], in0=py[:], in1=yo[i2][:].rearrange("p a b -> p (a b)"),
                                                                  op=ALU.add),
                          reads=["py", ("yo", i2)], writes=[("ytok", i2)])
                        for cc in range(4):
                            A("pe", lambda e, cc=cc, i2=i2: e.transpose(out=ptr[:, cc * 128:(cc + 1) * 128],
                                                                        in_=ytok[i2][:, cc * 128:(cc + 1) * 128], identity=ident[:]),
                              reads=[("ytok", i2), "ident"], writes=["ptr"])
                        A("dve", lambda e, i2=i2, bsl=bsl: e.tensor_tensor(
                            out=tmpg[i2][:], in0=xsT[:, :, bsl],
                            in1=pt[:, DSK + 4 * g:DSK + 4 * g + 4].unsqueeze(2).to_broadcast([128, 4, 128]), op=ALU.mult),
                          reads=sum(K_xsT, []) + [kpar], writes=[("tmpg", i2)])
                        A("dve", lambda e, i2=i2: e.tensor_tensor(out=tmpg[i2][:], in0=tmpg[i2][:],
                                                                  in1=ptr[:, 0:512].rearrange("p (a b) -> p a b", a=4), op=ALU.add),
                          reads=[("tmpg", i2), "ptr"], writes=[("tmpg", i2)])
                        A("dve", lambda e, i2=i2, bsl=bsl: e.tensor_tensor(out=szT[:, :, bsl], in0=szT[:, :, bsl], in1=tmpg[i2][:], op=ALU.mult),
                          reads=sum(K_szT, []) + [("tmpg", i2)], writes=sum(K_szT, []))
                    for cc in range(4):
                        q = sq[sqslot[0] % 2]
                        kq = ("sq", sqslot[0] % 2)
                        sqslot[0] += 1
                        A("act", lambda e, q=q, cc=cc: e.activation(out=q[:], in_=szT[:, cc, :], func=AF.Square),
                          reads=K_szT[cc], writes=[kq])
                        A("pe", lambda e, q=q, cc=cc: e.matmul(pstat[:], lhsT=ones_bf[:], rhs=q[:], start=(cc == 0), stop=(cc == 3)),
                          reads=[kq, "ones_bf"], writes=["pstat"])
                    rs_from_stat(512.0)
                    for cc in range(4):
                        A("dve", lambda e, cc=cc: e.scalar_tensor_tensor(out=mixT[:, 16 + 4 * g + cc, :], in0=szT[:, cc, :],
                                                                        scalar=pt[:, NW + 4 * g + cc:NW + 4 * g + cc + 1], in1=rs[:],
                                                                        op0=ALU.mult, op1=ALU.mult),
                          reads=K_szT[cc] + ["rs", kpar], writes=[("mixT", 16 + 4 * g + cc)])

                for g in range(4):
                    w = 2 ** (g + 1)
                    ms = mwslot[0] % 2
                    mwslot[0] += 1
                    S.dma("pool", f"mw{ms}", mw[ms][:], mixw[step, g].rearrange("(j p) d -> p j d", p=128), writes=[("mw", ms)])
                    for half in range(2):
                        slot = load_wblk(g * 512 + half * 256, 256)
                        for j in range(2):
                            cc = half * 2 + j
                            pci = g * 4 + cc
                            pp, kp = proj_chunk(wbuf[slot], j * 128, 128, ("wbuf", slot))
                            u = ub[cc % 2]
                            ku = K_ub[cc % 2]
                            A("act", lambda e, pp=pp, u=u: e.activation(out=u[:, 16:16 + T], in_=pp[:, :], func=AF.Copy),
                              reads=[kp], writes=ku)
                            A("dve", lambda e, u=u, pci=pci: e.tensor_copy(out=u[:, 0:16], in_=pcarry[:, pci, :]),
                              reads=[("pcarry", pci)], writes=ku)
                            A("dve", lambda e, u=u, pci=pci: e.tensor_copy(out=pcarry[:, pci, :], in_=u[:, T:T + 16]),
                              reads=ku, writes=[("pcarry", pci)])
                            cur, kcur = u, ku
                            lo = 0
                            for k in range(g + 1):
                                sh = 2 ** k
                                nxt, knxt = (pa, K_pa) if k % 2 == 0 else (pb, K_pb)
                                A("dve", lambda e, cur=cur, nxt=nxt, lo=lo, sh=sh: e.tensor_tensor(
                                    out=nxt[:, lo + sh:16 + T], in0=cur[:, lo + sh:16 + T], in1=cur[:, lo:16 + T - sh], op=ALU.add),
                                  reads=kcur, writes=knxt)
                                cur, kcur = nxt, knxt
                                lo += sh
                            A("dve", lambda e, cur=cur, u=u, cc=cc, w=w: e.scalar_tensor_tensor(
                                out=plT[:, cc, :], in0=cur[:, 16:16 + T], scalar=1.0 / w, in1=u[:, 16:16 + T],
                                op0=ALU.mult, op1=ALU.subtract),
                              reads=kcur + ku, writes=K_plT[cc])
                            if tt == 0:
                                A("dve", lambda e, cur=cur: e.tensor_tensor(out=tmp16[:], in0=cur[:, 16:32], in1=cstt[:, g * 16:(g + 1) * 16], op=ALU.mult),
                                  reads=kcur + ["cstt"], writes=["tmp16"])
                                A("dve", lambda e, u=u, cc=cc: e.tensor_tensor(out=plT[:, cc, 0:16], in0=tmp16[:], in1=u[:, 16:32], op=ALU.subtract),
                                  reads=["tmp16"] + ku, writes=K_plT[cc])
                    for half in range(2):
                        slot = load_wblk(2048 + g * 512 + half * 256, 256)
                        for j in range(2):
                            dch = half * 2 + j
                            pp, kp = proj_chunk(wbuf[slot], j * 128, 128, ("wbuf", slot))
                            sgt = sg[dch % 2]
                            ksg = K_sg[dch % 2]
                            A("act", lambda e, pp=pp, sgt=sgt: e.activation(out=sgt, in_=pp[:, :], func=AF.Silu), reads=[kp], writes=ksg)
                            b = pjslot[0] % 2
                            pjslot[0] += 1
                            for jj in range(4):
                                A("pe", lambda e, jj=jj, b=b, ms=ms, dch=dch: e.matmul(pj[b][:, :], lhsT=mw[ms][:, jj, dch * 128:(dch + 1) * 128],
                                                                                     rhs=plT[:, jj, :], start=(jj == 0), stop=(jj == 3)),
                                  reads=[("mw", ms)] + K_plT[jj], writes=[("pj", b)])
                            A("dve", lambda e, b=b, sgt=sgt, dch=dch: e.scalar_tensor_tensor(
                                out=mixT[:, 4 * g + dch, :], in0=pj[b][:, :], scalar=pt[:, PSC + 4 * g + dch:PSC + 4 * g + dch + 1],
                                in1=sgt, op0=ALU.mult, op1=ALU.mult),
                              reads=[("pj", b), kpar] + ksg, writes=[("mixT", 4 * g + dch)])

                for dc in range(NKC):
                    slot = woslot[0] % 2
                    woslot[0] += 1
                    S.dma("pool", f"wo{slot}", wo[slot][:], w_out[step, :, dc * 128:(dc + 1) * 128].rearrange("(ec p) f -> p ec f", p=128),
                          writes=[("wo", slot)])
                    b = pjslot[0] % 2
                    pjslot[0] += 1
                    for ec in range(32):
                        A("pe", lambda e, ec=ec, b=b, slot=slot: e.matmul(pj[b][:, :], lhsT=wo[slot][:, ec, :], rhs=mixT[:, ec, :],
                                                                         start=(ec == 0), stop=(ec == 31)),
                          reads=[("wo", slot), ("mixT", ec)], writes=[("pj", b)])
                    A("act", lambda e, b=b, dc=dc: e.activation(out=bigc[:, dc, :], in_=pj[b][:, :], func=AF.Copy),
                      reads=[("pj", b)], writes=K_big[dc])
                    q = sq[sqslot[0] % 2]
                    kq = ("sq", sqslot[0] % 2)
                    sqslot[0] += 1
                    A("act", lambda e, q=q, dc=dc: e.activation(out=q[:], in_=bigc[:, dc, :], func=AF.Square), reads=K_big[dc], writes=[kq])
                    A("pe", lambda e, q=q, dc=dc: e.matmul(pstat[:], lhsT=ones_bf[:], rhs=q[:], start=(dc == 0), stop=(dc == NKC - 1)),
                      reads=[kq, "ones_bf"], writes=["pstat"])
                rs_from_stat(float(D))
                outs = []
                for dc in range(NKC):
                    xs_ = xr[dc % 2]
                    S.dma("sp", f"xr{dc % 2}", xs_[:], src[:, dc, tsl], reads=[("xd", tt, dc)], writes=[("xr", dc % 2)])
                    A("dve", lambda e, dc=dc: e.scalar_tensor_tensor(out=bigc[:, dc, :], in0=bigc[:, dc, :],
                                                                    scalar=pt[:, POST + dc:POST + dc + 1], in1=rs[:],
                                                                    op0=ALU.mult, op1=ALU.mult),
                      reads=K_big[dc] + ["rs", kpar], writes=K_big[dc])
                    A("dve", lambda e, dc=dc, xs_=xs_: e.tensor_tensor(out=bigc[:, dc, :], in0=bigc[:, dc, :], in1=xs_[:], op=ALU.add),
                      reads=K_big[dc] + [("xr", dc % 2)], writes=K_big[dc])
                    outs.append(S.dma("sp", "xout", y_out[:, dc, tsl], bigc[:, dc, :], reads=K_big[dc], writes=[("xd", tt, dc)]))
                S.batch(outs)

        S.emit(nc, final_waits=["xout"])
        nops = {e: len(S.ops[e]) for e in ENGS}
    return nc, nops


def pack_params(inp, layer):
    p = np.zeros((128, NPAR), np.float32)

    def cols(v, n):
        return np.asarray(v, np.float32).reshape(n, 128).T

    p[:, PRE:PRE + 16] = cols(inp["pre_norm_w"][layer], 16)
    p[:, POST:POST + 16] = cols(inp["post_norm_w"][layer], 16)
    p[:, PSC:PSC + 16] = cols(inp["pool_scale"][layer], 16)
    cw = np.asarray(inp["conv_w"][layer], np.float32)
    p[:, CW:CW + 96] = cw.reshape(4, 24, 128).transpose(2, 1, 0).reshape(128, 96)
    p[:, CB:CB + 24] = cols(inp["conv_b"][layer], 24)
    p[:, NW:NW + 16] = cols(inp["ssd_norm_w"][layer], 16)
    p[:, DSK:DSK + 16] = cols(np.repeat(np.asarray(inp["d_skip"][layer], np.float32), 64), 16)
    p[0:32, DTB] = np.asarray(inp["dt_bias"][layer], np.float32)
    p[:, ALOG:ALOG + 32] = np.asarray(inp["a_log"][layer], np.float32)[None, :]
    return p


def make_cst(seq_start=True):
    c = np.zeros((128, 64), np.float32)
    for g in range(4):
        w = 2 ** (g + 1)
        for t in range(16):
            c[:, g * 16 + t] = 1.0 / (min(t + 1, w) if seq_start else w)
    return c


def to_xT(xb):
    nt = xb.shape[0]
    return np.ascontiguousarray(xb.T.reshape(NKC, 128, nt).transpose(1, 0, 2))


def from_xT(y):
    nt = y.shape[2]
    return np.ascontiguousarray(y.transpose(1, 0, 2).reshape(D, nt).T)


def kernel(**inputs):
    x = np.asarray(inputs["x"], np.float32)
    B, L, _ = x.shape
    depth = inputs["w_in"].shape[0]
    nc, _ = build(L, depth)
    pars = np.stack([pack_params(inputs, l) for l in range(depth)])
    shared = {
        "w_in": np.ascontiguousarray(inputs["w_in"], np.float32),
        "w_out": np.ascontiguousarray(inputs["w_out"], np.float32),
        "mixw": np.ascontiguousarray(inputs["pool_mix_w"], np.float32),
        "par": pars,
        "cst": make_cst(True),
    }
    in_maps = []
    for c in range(B):
        m = dict(shared)
        m["x"] = to_xT(x[c])
        in_maps.append(m)
    res = run_bass_kernel_spmd(nc, in_maps, core_ids=list(range(B)))
    out = np.stack([from_xT(np.asarray(res.results[b]["y"])) for b in range(B)])
    return out.astype(np.float32)
```

```python
import numpy as np
from contextlib import ExitStack
import concourse.bass as bass
import concourse.mybir as mybir
from concourse.bass_utils import run_bass_kernel_spmd

F32, BF16 = mybir.dt.float32, mybir.dt.bfloat16
AF = mybir.ActivationFunctionType
ALU = mybir.AluOpType

D = 2048
NKC = 16
T = 512
DIN = 9248
EPS = 1e-6
PRE, POST, PSC, CW, CB, NW, DSK, DTB, ALOG = 0, 16, 32, 48, 144, 168, 184, 200, 201
NPAR = 233

ENGS = ("pe", "act", "dve", "pool", "sp")
SEM_LIMIT = 8000


class Op:
    __slots__ = ("eng", "fn", "deps", "signal", "sig", "dma", "idx")

    def __init__(self, eng, fn):
        self.eng = eng
        self.fn = fn
        self.deps = []
        self.signal = False
        self.sig = None
        self.dma = None
        self.idx = 0


class _Rec:
    def __getattr__(self, name):
        def f(*a, **k):
            self.call = (name, a, k)
            return self
        return f


class Sched:
    def batch(self, ops):
        n = max(o.dma[1] for o in ops)
        for o in ops:
            o.dma = (o.dma[0], n)

    def __init__(self, same_engine_sync=True):
        self.ops = {e: [] for e in ENGS}
        self.last_w = {}
        self.readers = {}
        self.dma_cnt = {}
        self.same_engine_sync = same_engine_sync

    def _deps(self, op, reads, writes):
        deps = {}

        def add(p):
            if p is None or p is op:
                return
            if (p.dma is None and op.dma is None and p.eng == op.eng
                    and (op.eng == "pe" or not self.same_engine_sync)):
                return
            deps[id(p)] = p

        for b in reads:
            add(self.last_w.get(b))
        for b in writes:
            add(self.last_w.get(b))
            for r in self.readers.get(b, ()):
                add(r)
        best = {}
        out = []
        for p in deps.values():
            if p.dma is not None:
                out.append(p)
            else:
                q = best.get(p.eng)
                if q is None or p.idx > q.idx:
                    best[p.eng] = p
        out.extend(best.values())
        op.deps = out
        for b in writes:
            self.last_w[b] = op
            self.readers[b] = []
        for b in reads:
            self.readers.setdefault(b, []).append(op)

    def add(self, eng, fn, reads=(), writes=()):
        rec = _Rec()
        fn(rec)
        name, a, k = rec.call
        op = Op(eng, lambda e, name=name, a=a, k=k: getattr(e, name)(*a, **k))
        op.idx = len(self.ops[eng])
        self._deps(op, reads, writes)
        self.ops[eng].append(op)
        return op

    def dma(self, eng, semkey, out, in_, reads=(), writes=()):
        op = Op(eng, lambda e: e.dma_start(out=out, in_=in_))
        op.idx = len(self.ops[eng])
        n = self.dma_cnt.get(semkey, 0) + 16
        self.dma_cnt[semkey] = n
        op.dma = (semkey, n)
        self._deps(op, reads, writes)
        self.ops[eng].append(op)
        return op

    def emit(self, nc, final_waits=()):
        for e in ENGS:
            for op in self.ops[e]:
                for p in op.deps:
                    if p.dma is None:
                        p.signal = True
        nsem = {}
        for e in ENGS:
            k = 0
            for op in self.ops[e]:
                if op.dma is None and op.signal:
                    op.sig = (k // SEM_LIMIT, k % SEM_LIMIT + 1)
                    k += 1
            nsem[e] = (k + SEM_LIMIT - 1) // SEM_LIMIT
        with ExitStack() as st:
            sems = {e: [st.enter_context(nc.semaphore(f"s_{e}_{i}")) for i in range(nsem[e])] for e in ENGS}
            dsems = {k: st.enter_context(nc.semaphore(f"d_{k}")) for k in self.dma_cnt}
            block = st.enter_context(nc.Block())
            engmap = {"pe": "tensor", "act": "scalar", "dve": "vector", "pool": "gpsimd", "sp": "sync"}

            def make(e):
                def body(eng):
                    waited = {}
                    dwaited = {}
                    for op in self.ops[e]:
                        for p in op.deps:
                            if p.dma is not None:
                                k, v = p.dma
                                if dwaited.get(k, 0) < v:
                                    eng.wait_ge(dsems[k], v)
                                    dwaited[k] = v
                            else:
                                si, v = p.sig
                                key = (p.eng, si)
                                if waited.get(key, 0) < v:
                                    eng.wait_ge(sems[p.eng][si], v)
                                    waited[key] = v
                        ins = op.fn(eng)
                        if op.dma is not None:
                            ins.then_inc(dsems[op.dma[0]], 16)
                        elif op.signal:
                            ins.then_inc(sems[e][op.sig[0]], 1)
                    if e == "sp":
                        for k in final_waits:
                            eng.wait_ge(dsems[k], self.dma_cnt[k])
                return body

            for e in ENGS:
                getattr(block, engmap[e])(make(e))


def build(NT, NSTEP, same_engine_sync=True):
    assert NT % T == 0
    NTILE = NT // T
    nc = bass.Bass("TRN2", target_bir_lowering=False)
    x_in = nc.dram_tensor("x", [128, NKC, NT], F32, kind="ExternalInput").ap()
    y_out = nc.dram_tensor("y", [128, NKC, NT], F32, kind="ExternalOutput").ap()
    w_in = nc.dram_tensor("w_in", [NSTEP, D, DIN], F32, kind="ExternalInput").ap()
    w_out = nc.dram_tensor("w_out", [NSTEP, 2 * D, D], F32, kind="ExternalInput").ap()
    mixw = nc.dram_tensor("mixw", [NSTEP, 4, 512, 512], F32, kind="ExternalInput").ap()
    par = nc.dram_tensor("par", [NSTEP, 128, NPAR], F32, kind="ExternalInput").ap()
    cst = nc.dram_tensor("cst", [128, 64], F32, kind="ExternalInput").ap()
    wsc = nc.dram_tensor("wsc", [2, 37, 128, NKC, 256], BF16, kind="Internal").ap()
    wosc = nc.dram_tensor("wosc", [2, NKC, 128, 32, 128], BF16, kind="Internal").ap()
    mwsc = nc.dram_tensor("mwsc", [2, 4, 128, 4, 512], BF16, kind="Internal").ap()
    in_blocks = [(9216, 32)]
    for base_col in (8192, 8704):
        in_blocks += [(base_col, 256), (base_col + 256, 256)]
    for g in range(4):
        in_blocks += [(6144 + g * 512, 256), (6144 + g * 512 + 256, 256), (4096 + g * 512, 256), (4096 + g * 512 + 256, 256)]
    for g in range(4):
        in_blocks += [(g * 512, 256), (g * 512 + 256, 256), (2048 + g * 512, 256), (2048 + g * 512 + 256, 256)]
    blk_of = {c0: i for i, (c0, _) in enumerate(in_blocks)}

    with ExitStack() as st:
        def sb(name, shape, dt):
            return st.enter_context(nc.sbuf_tensor(name, shape, dt))

        def ps(name, shape, dt):
            return st.enter_context(nc.psum_tensor(name, shape, dt))

        S = Sched(same_engine_sync)
        A = S.add

        big = sb("big", [128, 8192], F32)
        bigc = big[:].rearrange("p (c t) -> p c t", c=16)
        szT = big[:, 0:2048].rearrange("p (c t) -> p c t", c=4)
        BT = big[:, 2048:3072].bitcast(BF16).rearrange("p (g t) -> p g t", g=4)
        CT = big[:, 3072:4096].bitcast(BF16).rearrange("p (g t) -> p g t", g=4)
        xsT = big[:, 4096:5120].bitcast(BF16).rearrange("p (c t) -> p c t", c=4)
        ub = [big[:, 0:528], big[:, 528:1056]]
        pa = big[:, 1056:1584]
        pb = big[:, 1584:2112]
        plT = big[:, 2112:3136].bitcast(BF16).rearrange("p (c t) -> p c t", c=4)
        sg = [big[:, 3136:3648], big[:, 3648:4160]]

        def segs(lo, hi):
            return [("big", i) for i in range(lo // 512, (hi - 1) // 512 + 1)]

        K_szT = [segs(c * 512, (c + 1) * 512) for c in range(4)]
        K_BT = [segs(2048 + g * 256, 2048 + (g + 1) * 256) for g in range(4)]
        K_CT = [segs(3072 + g * 256, 3072 + (g + 1) * 256) for g in range(4)]
        K_xsT = [segs(4096 + c * 256, 4096 + (c + 1) * 256) for c in range(4)]
        K_ub = [segs(0, 528), segs(528, 1056)]
        K_pa = segs(1056, 1584)
        K_pb = segs(1584, 2112)
        K_plT = [segs(2112 + c * 256, 2112 + (c + 1) * 256) for c in range(4)]
        K_sg = [segs(3136, 3648), segs(3648, 4160)]
        K_big = [segs(c * 512, (c + 1) * 512) for c in range(16)]

        hT = sb("hT", [128, NKC, T], BF16)
        mixT = sb("mixT", [128, 32, T], BF16)
        NWB = 3
        wbuf = [sb(f"wbuf{i}", [128, NKC, 256], BF16) for i in range(NWB)]
        wo = [sb(f"wo{i}", [128, 32, 128], BF16) for i in range(2)]
        mw = [sb(f"mw{i}", [128, 4, 512], BF16) for i in range(2)]
        wdt = sb("wdt", [128, NKC, 32], BF16)
        part = [sb(f"par{i}", [128, NPAR], F32) for i in range(2)]
        cstt = sb("cstt", [128, 64], F32)
        ident = sb("ident", [128, 128], BF16)
        identF = sb("identF", [128, 128], F32)
        mle = sb("mle", [128, 64], F32)
        mgt2 = sb("mgt2", [128, 128], F32)
        mle2 = sb("mle2", [128, 128], F32)
        onesc = sb("onesc", [128, 2, 128], F32)
        ones_bf = sb("ones_bf", [128, 128], BF16)
        a_bc = sb("a_bc", [128, 32], F32)
        rs = sb("rs", [128, T], F32)
        sq = [sb(f"sq{i}", [128, T], BF16) for i in range(2)]
        raw = [sb(f"raw{i}", [128, T + 3], F32) for i in range(2)]
        acc = [sb(f"acc{i}", [128, T], F32) for i in range(2)]
        carry = sb("carry", [128, 24, 3], F32)
        pcarry = sb("pcarry", [128, 16, 16], F32)
        dtT = sb("dtT", [32, T], F32)
        dt_tok = sb("dt_tok", [128, 4, 32], F32)
        adt_tok = sb("adt_tok", [128, 4, 32], F32)
        Sst = sb("Sst", [128, 4, 512], F32)
        Sbf = sb("Sbf", [128, 4, 512], BF16)
        scm = [sb(f"scm{i}", [128, 64], F32) for i in range(2)]
        Ap = [sb(f"Ap{i}", [128, 8, 64], F32) for i in range(2)]
        Eb = [sb(f"E{i}", [128, 8, 64], F32) for i in range(2)]
        MT = [sb(f"MT{i}", [128, 8, 64], BF16) for i in range(2)]
        xdt = [sb(f"xdt{i}", [128, 8, 64], BF16) for i in range(2)]
        xdtd = [sb(f"xdtd{i}", [128, 8, 64], BF16) for i in range(2)]
        Btok = [sb(f"Btok{i}", [128, 128], BF16) for i in range(2)]
        decacs = [sb(f"decacs{i}", [128, 24], F32) for i in range(2)]
        yo = [sb(f"yo{i}", [128, 8, 64], F32) for i in range(2)]
        ytok = [sb(f"ytok{i}", [128, 512], BF16) for i in range(2)]
        tmpg = [sb(f"tmpg{i}", [128, 4, 128], F32) for i in range(2)]
        tmp16 = sb("tmp16", [128, 16], F32)
        xr = [sb(f"xr{i}", [128, T], F32) for i in range(2)]

        pj = [ps(f"pj{i}", [128, 512], F32) for i in range(2)]
        pstat = ps("pstat", [128, 512], F32)
        psmall = ps("psmall", [128, 512], F32)
        ptr = ps("ptr", [128, 1024], BF16)
        py = ps("py", [128, 512], F32)
        pyoff = ps("pyoff", [128, 512], F32)
        pst = ps("pst", [128, 512], F32)

        def conv_items(step):
            par_ = step % 2
            items = []
            for nb, (c0, w_) in enumerate(in_blocks[:21]):
                items.append((wsc[par_, nb, :, :, 0:w_], w_in[step, :, c0:c0 + w_].rearrange("(kc p) f -> p kc f", p=128), ("wsc", par_, nb)))
            for g in range(4):
                items.append((mwsc[par_, g], mixw[step, g].rearrange("(j p) d -> p j d", p=128), ("mwsc", par_, g)))
                for nb in range(21 + 4 * g, 25 + 4 * g):
                    c0, w_ = in_blocks[nb]
                    items.append((wsc[par_, nb, :, :, 0:w_], w_in[step, :, c0:c0 + w_].rearrange("(kc p) f -> p kc f", p=128), ("wsc", par_, nb)))
            for dc in range(NKC):
                items.append((wosc[par_, dc], w_out[step, :, dc * 128:(dc + 1) * 128].rearrange("(ec p) f -> p ec f", p=128), ("wosc", par_, dc)))
            return items

        def emit_conv(step, items, extra_reads=()):
            return [S.dma("pool", f"cv{step % 2}", dst, src_, reads=list(extra_reads), writes=[key]) for dst, src_, key in items]

        A("pool", lambda e: e.memset(ident[:], 1.0), writes=["ident"])
        A("pool", lambda e: e.affine_select(out=ident[:], in_=ident[:], pattern=[[1, 128]], compare_op=ALU.is_equal,
                                            fill=0.0, base=0, channel_multiplier=-1), reads=["ident"], writes=["ident"])
        A("pool", lambda e: e.memset(identF[:], 1.0), writes=["identF"])
        A("pool", lambda e: e.affine_select(out=identF[:], in_=identF[:], pattern=[[1, 128]], compare_op=ALU.is_equal,
                                            fill=0.0, base=0, channel_multiplier=-1), reads=["identF"], writes=["identF"])
        A("pool", lambda e: e.memset(mle[:], 1.0), writes=["mle"])
        for c in range(2):
            v = mle[c * 64:(c + 1) * 64, :]
            A("pool", lambda e, v=v: e.affine_select(out=v, in_=v, pattern=[[1, 64]], compare_op=ALU.is_ge, fill=0.0,
                                                     base=0, channel_multiplier=-1), reads=["mle"], writes=["mle"])
        A("pool", lambda e: e.memset(mgt2[:], 0.0), writes=["mgt2"])
        A("pool", lambda e: e.memset(mle2[:], 0.0), writes=["mle2"])
        A("pool", lambda e: e.memset(onesc[:], 0.0), writes=["onesc"])
        for c in range(2):
            v = mgt2[c * 64:(c + 1) * 64, c * 64:(c + 1) * 64]
            A("pool", lambda e, v=v: e.memset(v, 1.0), reads=["mgt2"], writes=["mgt2"])
            A("pool", lambda e, v=v: e.affine_select(out=v, in_=v, pattern=[[-1, 64]], compare_op=ALU.is_ge, fill=0.0,
                                                     base=-1, channel_multiplier=1), reads=["mgt2"], writes=["mgt2"])
            v2 = mle2[c * 64:(c + 1) * 64, c * 64:(c + 1) * 64]
            A("pool", lambda e, v2=v2: e.memset(v2, 1.0), reads=["mle2"], writes=["mle2"])
            A("pool", lambda e, v2=v2: e.affine_select(out=v2, in_=v2, pattern=[[1, 64]], compare_op=ALU.is_ge, fill=0.0,
                                                       base=0, channel_multiplier=-1), reads=["mle2"], writes=["mle2"])
            v3 = onesc[c * 64:(c + 1) * 64, c, :]
            A("pool", lambda e, v3=v3: e.memset(v3, 1.0), reads=["onesc"], writes=["onesc"])
        A("pool", lambda e: e.memset(ones_bf[:], 1.0), writes=["ones_bf"])
        S.dma("sp", "cst", cstt[:], cst, writes=["cstt"])

        wslot = [0]
        woslot = [0]
        mwslot = [0]
        pjslot = [0]
        sqslot = [0]
        rawslot = [0]
        pend_silu = []
        pend_stat = []

        def rs_from_stat(n_feat, key_rs="rs"):
            A("act", lambda e: e.activation(out=rs[:], in_=pstat[:], func=AF.Ln, bias=EPS, scale=1.0 / n_feat),
              reads=["pstat"], writes=[key_rs])
            A("act", lambda e: e.activation(out=rs[:], in_=rs[:], func=AF.Exp, scale=-0.5), reads=[key_rs], writes=[key_rs])

        S.batch(emit_conv(0, conv_items(0)))
        for step in range(NSTEP):
            pt = part[step % 2]
            kpar = ("par", step % 2)
            wpar = step % 2
            nxt_items = conv_items(step + 1) if step + 1 < NSTEP else []
            nxt_ops = []
            per_tile = (len(nxt_items) + NTILE - 1) // NTILE
            S.dma("sp", f"par{step % 2}", pt[:], par[step], writes=[kpar])
            A("act", lambda e, pt=pt: e.activation(out=a_bc[:], in_=pt[:, ALOG:ALOG + 32], func=AF.Exp), reads=[kpar], writes=["a_bc"])
            A("act", lambda e: e.activation(out=a_bc[:], in_=a_bc[:], func=AF.Copy, scale=-1.0), reads=["a_bc"], writes=["a_bc"])
            A("pool", lambda e: e.memset(Sst[:], 0.0), writes=[("S", g) for g in range(4)])
            A("pool", lambda e: e.memset(Sbf[:], 0.0), writes=[("Sbf", g) for g in range(4)])
            A("pool", lambda e: e.memset(carry[:], 0.0), writes=[("carry", i) for i in range(24)])
            A("pool", lambda e: e.memset(pcarry[:], 0.0), writes=[("pcarry", i) for i in range(16)])
            src = x_in if step == 0 else y_out

            for tt in range(NTILE):
                tsl = slice(tt * T, (tt + 1) * T)
                S.batch([S.dma("sp", "xin", bigc[:, kc, :], src[:, kc, tsl], reads=[("xd", tt, kc)], writes=K_big[kc])
                         for kc in range(NKC)])
                for kc in range(NKC):
                    q = sq[sqslot[0] % 2]
                    kq = ("sq", sqslot[0] % 2)
                    sqslot[0] += 1
                    A("act", lambda e, q=q, kc=kc: e.activation(out=q[:], in_=bigc[:, kc, :], func=AF.Square),
                      reads=K_big[kc], writes=[kq])
                    A("pe", lambda e, q=q, kc=kc: e.matmul(pstat[:], lhsT=ones_bf[:], rhs=q[:], start=(kc == 0), stop=(kc == NKC - 1)),
                      reads=[kq, "ones_bf"], writes=["pstat"])
                rs_from_stat(float(D))
                for kc in range(NKC):
                    A("dve", lambda e, kc=kc, pt=pt: e.scalar_tensor_tensor(out=hT[:, kc, :], in0=bigc[:, kc, :],
                                                                          scalar=pt[:, PRE + kc:PRE + kc + 1], in1=rs[:],
                                                                          op0=ALU.mult, op1=ALU.mult),
                      reads=K_big[kc] + ["rs", kpar], writes=[("hT", kc)])

                def load_wblk(c0, width):
                    slot = wslot[0] % NWB
                    wslot[0] += 1
                    nb = blk_of[c0]
                    S.dma("sp", f"w{slot}", wbuf[slot][:, :, 0:width], wsc[wpar, nb, :, :, 0:width],
                          reads=[("wsc", wpar, nb)], writes=[("wbuf", slot)])
                    return slot

                def proj_chunk(wt, f0, fw, kw):
                    b = pjslot[0] % 2
                    pjslot[0] += 1
                    for kc in range(NKC):
                        A("pe", lambda e, kc=kc, b=b: e.matmul(pj[b][0:fw, :], lhsT=wt[:, kc, f0:f0 + fw], rhs=hT[:, kc, :],
                                                                start=(kc == 0), stop=(kc == NKC - 1)),
                          reads=[kw, ("hT", kc)], writes=[("pj", b)])
                    return pj[b], ("pj", b)

                def flush_silu():
                    if pend_silu:
                        ac_, ka_, dst_, kdst_ = pend_silu.pop()
                        A("act", lambda e: e.activation(out=dst_, in_=ac_[:], func=AF.Silu), reads=[ka_], writes=kdst_)

                def conv_chunk(pp, kp, ci, dst, kdst):
                    r = raw[rawslot[0] % 2]
                    ac = acc[rawslot[0] % 2]
                    kr = ("raw", rawslot[0] % 2)
                    krh = ("rawh", rawslot[0] % 2)
                    ka = ("acc", rawslot[0] % 2)
                    rawslot[0] += 1
                    A("act", lambda e: e.activation(out=r[:, 0:3], in_=carry[:, ci, :], func=AF.Copy), reads=[("carry", ci)], writes=[krh])
                    A("act", lambda e: e.activation(out=r[:, 3:3 + T], in_=pp[:, :], func=AF.Copy), reads=[kp], writes=[kr])
                    A("act", lambda e: e.activation(out=carry[:, ci, :], in_=r[:, T:T + 3], func=AF.Copy), reads=[kr], writes=[("carry", ci)])
                    flush_silu()
                    A("dve", lambda e: e.tensor_scalar(out=ac[:], in0=r[:, 0:T], scalar1=pt[:, CW + ci * 4:CW + ci * 4 + 1],
                                                       scalar2=pt[:, CB + ci:CB + ci + 1], op0=ALU.mult, op1=ALU.add),
                      reads=[kr, krh, kpar], writes=[ka])
                    for k in range(1, 4):
                        A("dve", lambda e, k=k: e.scalar_tensor_tensor(out=ac[:], in0=r[:, k:k + T],
                                                                       scalar=pt[:, CW + ci * 4 + k:CW + ci * 4 + k + 1],
                                                                       in1=ac[:], op0=ALU.mult, op1=ALU.add),
                          reads=[kr, krh, ka, kpar], writes=[ka])
                    pend_silu.append((ac, ka, dst, kdst))

                S.dma("sp", "wdt", wdt[:], wsc[wpar, 0, :, :, 0:32], reads=[("wsc", wpar, 0)], writes=["wdt"])
                pp, kp = proj_chunk(wdt, 0, 32, "wdt")
                A("act", lambda e, pp=pp: e.activation(out=dtT[:], in_=pp[0:32, :], func=AF.Exp, bias=pt[0:32, DTB:DTB + 1]),
                  reads=[kp, kpar], writes=["dtT"])
                A("act", lambda e: e.activation(out=dtT[:], in_=dtT[:], func=AF.Ln, bias=1.0), reads=["dtT"], writes=["dtT"])
                for blk in range(4):
                    A("pe", lambda e, blk=blk: e.transpose(out=psmall[:, 160:192], in_=dtT[0:32, blk * 128:(blk + 1) * 128],
                                                           identity=identF[0:32, 0:32]),
                      reads=["dtT", "identF"], writes=["psmall"])
                    A("dve", lambda e, blk=blk: e.tensor_copy(out=dt_tok[:, blk, :], in_=psmall[:, 160:192]),
                      reads=["psmall"], writes=[("dt_tok", blk)])
                    A("dve", lambda e, blk=blk: e.tensor_tensor(out=adt_tok[:, blk, :], in0=dt_tok[:, blk, :], in1=a_bc[:], op=ALU.mult),
                      reads=[("dt_tok", blk), "a_bc"], writes=[("adt_tok", blk)])
                for which, base_col, dstT, kd, ci0 in (("B", 8192, BT, K_BT, 16), ("C", 8704, CT, K_CT, 20)):
                    for half in range(2):
                        slot = load_wblk(base_col + half * 256, 256)
                        for j in range(2):
                            g = half * 2 + j
                            pp, kp = proj_chunk(wbuf[slot], j * 128, 128, ("wbuf", slot))
                            conv_chunk(pp, kp, ci0 + g, dstT[:, g, :], kd[g])
                flush_silu()

                for g in range(4):
                    for half in range(2):
                        slot = load_wblk(6144 + g * 512 + half * 256, 256)
                        for j in range(2):
                            cc = half * 2 + j
                            pp, kp = proj_chunk(wbuf[slot], j * 128, 128, ("wbuf", slot))
                            conv_chunk(pp, kp, g * 4 + cc, xsT[:, cc, :], K_xsT[cc])
                    flush_silu()
                    for half in range(2):
                        slot = load_wblk(4096 + g * 512 + half * 256, 256)
                        for j in range(2):
                            cc = half * 2 + j
                            pp, kp = proj_chunk(wbuf[slot], j * 128, 128, ("wbuf", slot))
                            A("act", lambda e, pp=pp, cc=cc: e.activation(out=szT[:, cc, :], in_=pp[:, :], func=AF.Silu),
                              reads=[kp], writes=K_szT[cc])
                    for blk in range(4):
                        i2 = blk % 2
                        bsl = slice(blk * 128, (blk + 1) * 128)
                        hs = slice(g * 8, (g + 1) * 8)
                        A("pe", lambda e, bsl=bsl: e.matmul(psmall[:, 0:128], lhsT=BT[:, g, bsl], rhs=CT[:, g, bsl], start=True, stop=True),
                          reads=K_BT[g] + K_CT[g], writes=["psmall"])
                        for c in range(2):
                            A("pe", lambda e, c=c, blk=blk, hs=hs: e.matmul(psmall[:, 128 + c * 8:136 + c * 8], lhsT=onesc[:, c, :],
                                                                           rhs=adt_tok[:, blk, hs], start=True, stop=True),
                              reads=[("adt_tok", blk), "onesc"], writes=["psmall"])
                        A("pe", lambda e, blk=blk, hs=hs: e.matmul(psmall[:, 144:152], lhsT=mle2[:], rhs=adt_tok[:, blk, hs], start=True, stop=True),
                          reads=[("adt_tok", blk), "mle2"], writes=["psmall"])
                        for c in range(2):
                            psl = slice(c * 64, (c + 1) * 64)
                            A("dve", lambda e, psl=psl, c=c, i2=i2: e.tensor_tensor(out=scm[i2][psl, :], in0=psmall[psl, c * 64:(c + 1) * 64],
                                                                                  in1=mle[psl, :], op=ALU.mult),
                              reads=["psmall", "mle"], writes=[("scm", i2)])
                        A("act", lambda e, i2=i2: e.activation(out=decacs[i2][:], in_=psmall[:, 128:152], func=AF.Exp),
                          reads=["psmall"], writes=[("decacs", i2)])
                        A("dve", lambda e, blk=blk, i2=i2, hs=hs: e.tensor_tensor(
                            out=Ap[i2][:], in0=adt_tok[:, blk, hs].unsqueeze(2).to_broadcast([128, 8, 64]),
                            in1=mle[:].unsqueeze(1).to_broadcast([128, 8, 64]), op=ALU.mult),
                          reads=[("adt_tok", blk), "mle"], writes=[("Ap", i2)])
                        A("pe", lambda e, i2=i2: e.matmul(pstat[:], lhsT=mgt2[:], rhs=Ap[i2][:].rearrange("p a b -> p (a b)"),
                                                          start=True, stop=True),
                          reads=[("Ap", i2), "mgt2"], writes=["pstat"])
                        A("act", lambda e, i2=i2: e.activation(out=Eb[i2][:].rearrange("p a b -> p (a b)"), in_=pstat[:], func=AF.Exp),
                          reads=["pstat"], writes=[("E", i2)])
                        A("dve", lambda e, i2=i2: e.tensor_tensor(out=MT[i2][:], in0=Eb[i2][:],
                                                                  in1=scm[i2][:].unsqueeze(1).to_broadcast([128, 8, 64]), op=ALU.mult),
                          reads=[("E", i2), ("scm", i2)], writes=[("MT", i2)])
                        for cc in range(4):
                            A("pe", lambda e, cc=cc, bsl=bsl: e.transpose(out=ptr[:, cc * 128:(cc + 1) * 128], in_=xsT[:, cc, bsl], identity=ident[:]),
                              reads=K_xsT[cc] + ["ident"], writes=["ptr"])
                        A("pe", lambda e, bsl=bsl: e.transpose(out=ptr[:, 512:640], in_=BT[:, g, bsl], identity=ident[:]),
                          reads=K_BT[g] + ["ident"], writes=["ptr"])
                        A("dve", lambda e, i2=i2, blk=blk, hs=hs: e.tensor_tensor(
                            out=xdt[i2][:], in0=ptr[:, 0:512].rearrange("p (a b) -> p a b", a=8),
                            in1=dt_tok[:, blk, hs].unsqueeze(2).to_broadcast([128, 8, 64]), op=ALU.mult),
                          reads=["ptr", ("dt_tok", blk)], writes=[("xdt", i2)])
                        A("dve", lambda e, i2=i2: e.tensor_tensor(
                            out=xdtd[i2][:], in0=xdt[i2][:], in1=Eb[i2][:, :, 63:64].to_broadcast([128, 8, 64]), op=ALU.mult),
                          reads=[("xdt", i2), ("E", i2)], writes=[("xdtd", i2)])
                        A("act", lambda e, i2=i2: e.activation(out=Btok[i2][:], in_=ptr[:, 512:640], func=AF.Copy),
                          reads=["ptr"], writes=[("Btok", i2)])
                        for c in range(2):
                            psl = slice(c * 64, (c + 1) * 64)
                            for r in range(8):
                                A("pe", lambda e, psl=psl, c=c, r=r, i2=i2: e.matmul(
                                    py[psl, r * 64:(r + 1) * 64], lhsT=MT[i2][psl, r, :], rhs=xdt[i2][psl, r, :],
                                    start=True, stop=True, tile_position=(c * 64, c * 64)),
                                  reads=[("MT", i2), ("xdt", i2)], writes=["py"])
                        for c in range(2):
                            psl = slice(c * 64, (c + 1) * 64)
                            csl = slice(blk * 128 + c * 64, blk * 128 + (c + 1) * 64)
                            A("pe", lambda e, psl=psl, csl=csl, c=c: e.matmul(pyoff[psl, :], lhsT=CT[:, g, csl], rhs=Sbf[:, g, :],
                                                                              start=True, stop=True, tile_position=(0, c * 64)),
                              reads=K_CT[g] + [("Sbf", g)], writes=["pyoff"])
                            A("pe", lambda e, psl=psl, i2=i2, c=c: e.matmul(pst[:], lhsT=Btok[i2][psl, :],
                                                                            rhs=xdtd[i2][psl, :, :].rearrange("p a b -> p (a b)"),
                                                                            start=True, stop=True, tile_position=(c * 64, 0)),
                              reads=[("Btok", i2), ("xdtd", i2)], writes=["pst"])
                            Sg = Sst[:, g, :].rearrange("p (a b) -> p a b", a=8)
                            A("dve", lambda e, Sg=Sg, i2=i2, c=c: e.tensor_tensor(
                                out=Sg, in0=Sg, in1=decacs[i2][:, c * 8:(c + 1) * 8].unsqueeze(2).to_broadcast([128, 8, 64]), op=ALU.mult),
                              reads=[("S", g), ("decacs", i2)], writes=[("S", g)])
                            A("dve", lambda e: e.tensor_tensor(out=Sst[:, g, :], in0=Sst[:, g, :], in1=pst[:], op=ALU.add),
                              reads=[("S", g), "pst"], writes=[("S", g)])
                            A("act", lambda e: e.activation(out=Sbf[:, g, :], in_=Sst[:, g, :], func=AF.Copy),
                              reads=[("S", g)], writes=[("Sbf", g)])
                        A("dve", lambda e, i2=i2: e.tensor_tensor(
                            out=yo[i2][:], in0=pyoff[:, :].rearrange("p (a b) -> p a b", a=8),
                            in1=decacs[i2][:, 16:24].unsqueeze(2).to_broadcast([128, 8, 64]), op=ALU.mult),
                          reads=["pyoff", ("decacs", i2)], writes=[("yo", i2)])
                        A("dve", lambda e, i2=i2: e.tensor_tensor(out=ytok[i2][:], in0=py[:], in1=yo[i2][:].rearrange("p a b -> p (a b)"),
                                                                  op=ALU.add),
                          reads=["py", ("yo", i2)], writes=[("ytok", i2)])
                        for cc in range(4):
                            A("pe", lambda e, cc=cc, i2=i2: e.transpose(out=ptr[:, cc * 128:(cc + 1) * 128],
                                                                        in_=ytok[i2][:, cc * 128:(cc + 1) * 128], identity=ident[:]),
                              reads=[("ytok", i2), "ident"], writes=["ptr"])
                        A("dve", lambda e, i2=i2, bsl=bsl: e.tensor_tensor(
                            out=tmpg[i2][:], in0=xsT[:, :, bsl],
                            in1=pt[:, DSK + 4 * g:DSK + 4 * g + 4].unsqueeze(2).to_broadcast([128, 4, 128]), op=ALU.mult),
                          reads=sum(K_xsT, []) + [kpar], writes=[("tmpg", i2)])
                        A("dve", lambda e, i2=i2: e.tensor_tensor(out=tmpg[i2][:], in0=tmpg[i2][:],
                                                                  in1=ptr[:, 0:512].rearrange("p (a b) -> p a b", a=4), op=ALU.add),
                          reads=[("tmpg", i2), "ptr"], writes=[("tmpg", i2)])
                        A("dve", lambda e, i2=i2, bsl=bsl: e.tensor_tensor(out=szT[:, :, bsl], in0=szT[:, :, bsl], in1=tmpg[i2][:], op=ALU.mult),
                          reads=sum(K_szT, []) + [("tmpg", i2)], writes=sum(K_szT, []))
                    for cc in range(4):
                        q = sq[sqslot[0] % 2]
                        kq = ("sq", sqslot[0] % 2)
                        sqslot[0] += 1
                        A("act", lambda e, q=q, cc=cc: e.activation(out=q[:], in_=szT[:, cc, :], func=AF.Square),
                          reads=K_szT[cc], writes=[kq])
                        A("pe", lambda e, q=q, cc=cc: e.matmul(pstat[:], lhsT=ones_bf[:], rhs=q[:], start=(cc == 0), stop=(cc == 3)),
                          reads=[kq, "ones_bf"], writes=["pstat"])
                    rs_from_stat(512.0)
                    for cc in range(4):
                        A("dve", lambda e, cc=cc: e.scalar_tensor_tensor(out=mixT[:, 16 + 4 * g + cc, :], in0=szT[:, cc, :],
                                                                        scalar=pt[:, NW + 4 * g + cc:NW + 4 * g + cc + 1], in1=rs[:],
                                                                        op0=ALU.mult, op1=ALU.mult),
                          reads=K_szT[cc] + ["rs", kpar], writes=[("mixT", 16 + 4 * g + cc)])

                for g in range(4):
                    w = 2 ** (g + 1)
                    ms = mwslot[0] % 2
                    mwslot[0] += 1
                    S.dma("sp", f"mw{ms}", mw[ms][:], mwsc[wpar, g], reads=[("mwsc", wpar, g)], writes=[("mw", ms)])
                    for half in range(2):
                        slot = load_wblk(g * 512 + half * 256, 256)
                        for j in range(2):
                            cc = half * 2 + j
                            pci = g * 4 + cc
                            pp, kp = proj_chunk(wbuf[slot], j * 128, 128, ("wbuf", slot))
                            u = ub[cc % 2]
                            ku = K_ub[cc % 2]
                            A("act", lambda e, pp=pp, u=u: e.activation(out=u[:, 16:16 + T], in_=pp[:, :], func=AF.Copy),
                              reads=[kp], writes=ku)
                            A("act", lambda e, u=u, pci=pci: e.activation(out=u[:, 0:16], in_=pcarry[:, pci, :], func=AF.Copy),
                              reads=[("pcarry", pci)], writes=ku)
                            A("act", lambda e, u=u, pci=pci: e.activation(out=pcarry[:, pci, :], in_=u[:, T:T + 16], func=AF.Copy),
                              reads=ku, writes=[("pcarry", pci)])
                            cur, kcur = u, ku
                            lo = 0
                            for k in range(g + 1):
                                sh = 2 ** k
                                nxt, knxt = (pa, K_pa) if k % 2 == 0 else (pb, K_pb)
                                A("dve", lambda e, cur=cur, nxt=nxt, lo=lo, sh=sh: e.tensor_tensor(
                                    out=nxt[:, lo + sh:16 + T], in0=cur[:, lo + sh:16 + T], in1=cur[:, lo:16 + T - sh], op=ALU.add),
                                  reads=kcur, writes=knxt)
                                cur, kcur = nxt, knxt
                                lo += sh
                            A("dve", lambda e, cur=cur, u=u, cc=cc, w=w: e.scalar_tensor_tensor(
                                out=plT[:, cc, :], in0=cur[:, 16:16 + T], scalar=1.0 / w, in1=u[:, 16:16 + T],
                                op0=ALU.mult, op1=ALU.subtract),
                              reads=kcur + ku, writes=K_plT[cc])
                            if tt == 0:
                                A("dve", lambda e, cur=cur: e.tensor_tensor(out=tmp16[:], in0=cur[:, 16:32], in1=cstt[:, g * 16:(g + 1) * 16], op=ALU.mult),
                                  reads=kcur + ["cstt"], writes=["tmp16"])
                                A("dve", lambda e, u=u, cc=cc: e.tensor_tensor(out=plT[:, cc, 0:16], in0=tmp16[:], in1=u[:, 16:32], op=ALU.subtract),
                                  reads=["tmp16"] + ku, writes=K_plT[cc])
                    for half in range(2):
                        slot = load_wblk(2048 + g * 512 + half * 256, 256)
                        for j in range(2):
                            dch = half * 2 + j
                            pp, kp = proj_chunk(wbuf[slot], j * 128, 128, ("wbuf", slot))
                            sgt = sg[dch % 2]
                            ksg = K_sg[dch % 2]
                            A("act", lambda e, pp=pp, sgt=sgt: e.activation(out=sgt, in_=pp[:, :], func=AF.Silu), reads=[kp], writes=ksg)
                            b = pjslot[0] % 2
                            pjslot[0] += 1
                            for jj in range(4):
                                A("pe", lambda e, jj=jj, b=b, ms=ms, dch=dch: e.matmul(pj[b][:, :], lhsT=mw[ms][:, jj, dch * 128:(dch + 1) * 128],
                                                                                     rhs=plT[:, jj, :], start=(jj == 0), stop=(jj == 3)),
                                  reads=[("mw", ms)] + K_plT[jj], writes=[("pj", b)])
                            A("dve", lambda e, b=b, sgt=sgt, dch=dch: e.scalar_tensor_tensor(
                                out=mixT[:, 4 * g + dch, :], in0=pj[b][:, :], scalar=pt[:, PSC + 4 * g + dch:PSC + 4 * g + dch + 1],
                                in1=sgt, op0=ALU.mult, op1=ALU.mult),
                              reads=[("pj", b), kpar] + ksg, writes=[("mixT", 4 * g + dch)])

                for dc in range(NKC):
                    slot = woslot[0] % 2
                    woslot[0] += 1
                    S.dma("sp", f"wo{slot}", wo[slot][:], wosc[wpar, dc], reads=[("wosc", wpar, dc)], writes=[("wo", slot)])
                    b = pjslot[0] % 2
                    pjslot[0] += 1
                    for ec in range(32):
                        A("pe", lambda e, ec=ec, b=b, slot=slot: e.matmul(pj[b][:, :], lhsT=wo[slot][:, ec, :], rhs=mixT[:, ec, :],
                                                                         start=(ec == 0), stop=(ec == 31)),
                          reads=[("wo", slot), ("mixT", ec)], writes=[("pj", b)])
                    A("act", lambda e, b=b, dc=dc: e.activation(out=bigc[:, dc, :], in_=pj[b][:, :], func=AF.Copy),
                      reads=[("pj", b)], writes=K_big[dc])
                    q = sq[sqslot[0] % 2]
                    kq = ("sq", sqslot[0] % 2)
                    sqslot[0] += 1
                    A("act", lambda e, q=q, dc=dc: e.activation(out=q[:], in_=bigc[:, dc, :], func=AF.Square), reads=K_big[dc], writes=[kq])
                    if pend_stat:
                        q_, kq_, dc_ = pend_stat.pop()
                        A("pe", lambda e: e.matmul(pstat[:], lhsT=ones_bf[:], rhs=q_[:], start=(dc_ == 0), stop=(dc_ == NKC - 1)),
                          reads=[kq_, "ones_bf"], writes=["pstat"])
                    pend_stat.append((q, kq, dc))
                q_, kq_, dc_ = pend_stat.pop()
                A("pe", lambda e: e.matmul(pstat[:], lhsT=ones_bf[:], rhs=q_[:], start=(dc_ == 0), stop=(dc_ == NKC - 1)),
                  reads=[kq_, "ones_bf"], writes=["pstat"])
                rs_from_stat(float(D))
                outs = []
                for dc in range(NKC):
                    xs_ = xr[dc % 2]
                    S.dma("sp", f"xr{dc % 2}", xs_[:], src[:, dc, tsl], reads=[("xd", tt, dc)], writes=[("xr", dc % 2)])
                    A("dve", lambda e, dc=dc: e.scalar_tensor_tensor(out=bigc[:, dc, :], in0=bigc[:, dc, :],
                                                                    scalar=pt[:, POST + dc:POST + dc + 1], in1=rs[:],
                                                                    op0=ALU.mult, op1=ALU.mult),
                      reads=K_big[dc] + ["rs", kpar], writes=K_big[dc])
                    A("dve", lambda e, dc=dc, xs_=xs_: e.tensor_tensor(out=bigc[:, dc, :], in0=bigc[:, dc, :], in1=xs_[:], op=ALU.add),
                      reads=K_big[dc] + [("xr", dc % 2)], writes=K_big[dc])
                    outs.append(S.dma("sp", "xout", y_out[:, dc, tsl], bigc[:, dc, :], reads=K_big[dc], writes=[("xd", tt, dc)]))
                S.batch(outs)
                if nxt_items:
                    sl_ = nxt_items[tt * per_tile:(tt + 1) * per_tile]
                    nxt_ops += emit_conv(step + 1, sl_, extra_reads=[("xd", tt, 0)])
                    if tt == NTILE - 1:
                        S.batch(nxt_ops)

        S.emit(nc, final_waits=["xout"])
        nops = {e: len(S.ops[e]) for e in ENGS}
    return nc, nops


def pack_params(inp, layer):
    p = np.zeros((128, NPAR), np.float32)

    def cols(v, n):
        return np.asarray(v, np.float32).reshape(n, 128).T

    p[:, PRE:PRE + 16] = cols(inp["pre_norm_w"][layer], 16)
    p[:, POST:POST + 16] = cols(inp["post_norm_w"][layer], 16)
    p[:, PSC:PSC + 16] = cols(inp["pool_scale"][layer], 16)
    cw = np.asarray(inp["conv_w"][layer], np.float32)
    p[:, CW:CW + 96] = cw.reshape(4, 24, 128).transpose(2, 1, 0).reshape(128, 96)
    p[:, CB:CB + 24] = cols(inp["conv_b"][layer], 24)
    p[:, NW:NW + 16] = cols(inp["ssd_norm_w"][layer], 16)
    p[:, DSK:DSK + 16] = cols(np.repeat(np.asarray(inp["d_skip"][layer], np.float32), 64), 16)
    p[0:32, DTB] = np.asarray(inp["dt_bias"][layer], np.float32)
    p[:, ALOG:ALOG + 32] = np.asarray(inp["a_log"][layer], np.float32)[None, :]
    return p


def make_cst(seq_start=True):
    c = np.zeros((128, 64), np.float32)
    for g in range(4):
        w = 2 ** (g + 1)
        for t in range(16):
            c[:, g * 16 + t] = 1.0 / (min(t + 1, w) if seq_start else w)
    return c


def to_xT(xb):
    nt = xb.shape[0]
    return np.ascontiguousarray(xb.T.reshape(NKC, 128, nt).transpose(1, 0, 2))


def from_xT(y):
    nt = y.shape[2]
    return np.ascontiguousarray(y.transpose(1, 0, 2).reshape(D, nt).T)


def kernel(**inputs):
    x = np.asarray(inputs["x"], np.float32)
    B, L, _ = x.shape
    depth = inputs["w_in"].shape[0]
    nc, _ = build(L, depth)
    pars = np.stack([pack_params(inputs, l) for l in range(depth)])
    shared = {
        "w_in": np.ascontiguousarray(inputs["w_in"], np.float32),
        "w_out": np.ascontiguousarray(inputs["w_out"], np.float32),
        "mixw": np.ascontiguousarray(inputs["pool_mix_w"], np.float32),
        "par": pars,
        "cst": make_cst(True),
    }
    in_maps = []
    for c in range(B):
        m = dict(shared)
        m["x"] = to_xT(x[c])
        in_maps.append(m)
    res = run_bass_kernel_spmd(nc, in_maps, core_ids=list(range(B)))
    out = np.stack([from_xT(np.asarray(res.results[b]["y"])) for b in range(B)])
    return out.astype(np.float32)
```
